# Optimizing a Trainium2 kernel written in Bass

```python
import jax, jax.numpy as jnp
from jax import lax
import numpy as np

D_MODEL = 1024
BATCH = 8
SEQ = 8192
DEPTH = 1
DEC_BATCH = 128
DEC_SEQ = 4
PAST_LEN = 8192
PAGE_SIZE = 128

D_MIX = D_MODEL
HEAD_DIM = 64
N_HEADS_A = (D_MIX // 2) // HEAD_DIM
D_A = N_HEADS_A * HEAD_DIM
D_B = D_MIX - D_A
CHUNK = 128
GROUP_B = 128
N_GROUPS_B = D_B // GROUP_B
D_FF = 2816
WINDOWS = (128, 512, 2048)
DILATIONS = (1, 4, 16)
W_MAX = max(WINDOWS)
ATTN_SCALE = HEAD_DIM ** -0.5
LN_EPS = 1e-5
NEG_INF = -1e30
DEEPNORM_ALPHA = (2.0 * DEPTH) ** 0.25
DEEPNORM_BETA = (8.0 * DEPTH) ** -0.25
FFN_HALF = 0.5

kernel_name = "hymba_longnet_gmlp_macaron_deepnorm_step"


def _layer_norm(x, g, b):
    xf = x.astype(jnp.float32)
    mu = xf.mean(-1, keepdims=True)
    var = jnp.square(xf - mu).mean(-1, keepdims=True)
    return ((xf - mu) * lax.rsqrt(var + LN_EPS) * g + b).astype(x.dtype)


def _rms_norm(x, g):
    xf = x.astype(jnp.float32)
    return (xf * lax.rsqrt(jnp.mean(xf * xf, -1, keepdims=True) + LN_EPS) * g).astype(x.dtype)


def _swiglu(x, w_in, w_out):
    gate, up = jnp.split(x @ w_in, 2, axis=-1)
    return (jax.nn.silu(gate) * up) @ w_out


def _attend_band(q, k, v, dilation, n_sub):
    B, S, H, Dh = q.shape
    span = dilation * n_sub
    L = -(-S // span) * span
    nb = L // span

    def blocks(t):
        t = jnp.pad(t.astype(jnp.float32), ((0, 0), (0, L - S), (0, 0), (0, 0)))
        return t.reshape(B, nb, n_sub, dilation, H, Dh)

    def with_prev(t):
        prev = jnp.concatenate([jnp.zeros_like(t[:, :1]), t[:, :-1]], axis=1)
        return jnp.concatenate([prev, t], axis=2)

    qb = blocks(q)
    kk = with_prev(blocks(k))
    vv = with_prev(blocks(v))
    s = jnp.einsum('bnirhd,bnjrhd->bnrhij', qb, kk) * ATTN_SCALE
    i = jnp.arange(n_sub)[:, None]
    j = jnp.arange(2 * n_sub)[None, :]
    diff = i + n_sub - j
    band = (diff >= 0) & (diff <= n_sub)
    exists = (jnp.arange(nb)[:, None, None] > 0) | (j[None] >= n_sub)
    mask = band[None] & exists
    s = jnp.where(mask[None, :, None, None], s, NEG_INF)
    m = s.max(-1, keepdims=True)
    e = jnp.exp(s - m)
    denom = e.sum(-1)
    o = jnp.einsum('bnrhij,bnjrhd->bnirhd', e, vv) / denom.transpose(0, 1, 4, 2, 3)[..., None]
    lse = (m[..., 0] + jnp.log(denom)).transpose(0, 1, 4, 2, 3)
    return o.reshape(B, L, H, Dh)[:, :S], lse.reshape(B, L, H)[:, :S]


def _attend_gathered(q, k_all, v_all, dilation, n_sub):
    T = q.shape[1]
    w_buf = k_all.shape[1] - T
    dist = jnp.arange(n_sub + 1) * dilation
    idx = w_buf + jnp.arange(T)[:, None] - dist[None, :]
    valid = idx >= 0
    idx = jnp.maximum(idx, 0)
    kg = jnp.take(k_all, idx, axis=1).astype(jnp.float32)
    vg = jnp.take(v_all, idx, axis=1).astype(jnp.float32)
    s = jnp.einsum('bthd,btjhd->bthj', q.astype(jnp.float32), kg) * ATTN_SCALE
    s = jnp.where(valid[None, :, None, :], s, NEG_INF)
    m = s.max(-1, keepdims=True)
    e = jnp.exp(s - m)
    denom = e.sum(-1)
    o = jnp.einsum('bthj,btjhd->bthd', e, vg) / denom[..., None]
    return o, m[..., 0] + jnp.log(denom)


def _combine_by_denominator(outs, lses):
    w = jax.nn.softmax(jnp.stack(lses), axis=0)
    return jnp.sum(w[..., None] * jnp.stack(outs), axis=0)


def _spatial_gating(u, v, w_s, b_s, g_v, b_v):
    B, S, _ = v.shape
    vn = _layer_norm(v, g_v, b_v)
    L = -(-S // CHUNK) * CHUNK
    vc = jnp.pad(vn, ((0, 0), (0, L - S), (0, 0))).reshape(B, L // CHUNK, CHUNK, N_GROUPS_B, GROUP_B)
    causal = jnp.tril(jnp.ones((CHUNK, CHUNK), dtype=bool))
    w = jnp.where(causal[None], w_s, 0.0)
    mix = jnp.einsum('gij,bnjgc->bnigc', w, vc) + b_s.T[None, None, :, :, None]
    return u * mix.reshape(B, L, D_B)[:, :S], vn


def _layer(x, p, cache_k, cache_v):
    (ffn1_w_in, ffn1_w_out, ln1_g, ln1_b, w_in, sgu_w, sgu_b, sgu_v_g, sgu_v_b,
     out_a_g, out_b_g, w_out, ln2_g, ln2_b, ffn2_w_in, ffn2_w_out, ln3_g, ln3_b) = p
    Bx, S, _ = x.shape
    x = _layer_norm(DEEPNORM_ALPHA * x + FFN_HALF * _swiglu(x, ffn1_w_in, ffn1_w_out), ln1_g, ln1_b)
    h = x @ w_in
    q, k, v, u_b, v_b = jnp.split(h, [D_A, 2 * D_A, 3 * D_A, 3 * D_A + D_B], axis=-1)
    q = q.reshape(Bx, S, N_HEADS_A, HEAD_DIM)
    k = k.reshape(Bx, S, N_HEADS_A, HEAD_DIM)
    v = v.reshape(Bx, S, N_HEADS_A, HEAD_DIM)
    outs, lses = [], []
    if cache_k is None:
        for win, dil in zip(WINDOWS, DILATIONS):
            o, l = _attend_band(q, k, v, dil, win // dil)
            outs.append(o)
            lses.append(l)
        keep = min(W_MAX, S)
        k_state, v_state = k[:, S - keep:], v[:, S - keep:]
    else:
        k_all = jnp.concatenate([cache_k.astype(k.dtype), k], axis=1)
        v_all = jnp.concatenate([cache_v.astype(v.dtype), v], axis=1)
        for win, dil in zip(WINDOWS, DILATIONS):
            o, l = _attend_gathered(q, k_all, v_all, dil, win // dil)
            outs.append(o)
            lses.append(l)
        k_state, v_state = k_all[:, S:], v_all[:, S:]
    o_a = _combine_by_denominator(outs, lses).astype(x.dtype).reshape(Bx, S, D_A)
    o_b, vn = _spatial_gating(u_b, v_b, sgu_w, sgu_b, sgu_v_g, sgu_v_b)
    mixed = jnp.concatenate([_rms_norm(o_a, out_a_g), _rms_norm(o_b, out_b_g)], axis=-1) @ w_out
    x = _layer_norm(DEEPNORM_ALPHA * x + mixed, ln2_g, ln2_b)
    x = _layer_norm(DEEPNORM_ALPHA * x + FFN_HALF * _swiglu(x, ffn2_w_in, ffn2_w_out), ln3_g, ln3_b)
    return x, k_state, v_state, vn


def setup_inputs(seed: int = 0) -> dict:
    key = jax.random.key(seed)
    ks = jax.random.split(key, 24)
    f32 = jnp.float32
    w_buf = min(W_MAX, PAST_LEN)

    def nrm(k, shape, scale=1.0):
        return jax.random.normal(k, shape, f32) * scale

    return {
        "x_prompt": nrm(ks[0], (BATCH, SEQ, D_MODEL)),
        "x_sample": nrm(ks[1], (DEC_BATCH, DEC_SEQ, D_MODEL)),
        "cache_k": nrm(ks[2], (DEPTH, DEC_BATCH, w_buf, N_HEADS_A, HEAD_DIM)),
        "cache_v": nrm(ks[3], (DEPTH, DEC_BATCH, w_buf, N_HEADS_A, HEAD_DIM)),
        "ffn1_w_in": nrm(ks[4], (DEPTH, D_MODEL, 2 * D_FF), D_MODEL ** -0.5),
        "ffn1_w_out": nrm(ks[5], (DEPTH, D_FF, D_MODEL), DEEPNORM_BETA * D_FF ** -0.5),
        "ln1_g": 1.0 + nrm(ks[6], (DEPTH, D_MODEL), 0.01),
        "ln1_b": nrm(ks[7], (DEPTH, D_MODEL), 0.01),
        "w_in": nrm(ks[8], (DEPTH, D_MODEL, 3 * D_A + 2 * D_B), D_MODEL ** -0.5),
        "sgu_w": nrm(ks[9], (DEPTH, N_GROUPS_B, CHUNK, CHUNK), CHUNK ** -0.5),
        "sgu_b": 1.0 + nrm(ks[10], (DEPTH, N_GROUPS_B, CHUNK), 0.01),
        "sgu_v_g": 1.0 + nrm(ks[11], (DEPTH, D_B), 0.01),
        "sgu_v_b": nrm(ks[12], (DEPTH, D_B), 0.01),
        "out_a_g": 1.0 + nrm(ks[13], (DEPTH, D_A), 0.01),
        "out_b_g": 1.0 + nrm(ks[14], (DEPTH, D_B), 0.01),
        "w_out": nrm(ks[15], (DEPTH, D_MIX, D_MODEL), DEEPNORM_BETA * D_MIX ** -0.5),
        "ln2_g": 1.0 + nrm(ks[16], (DEPTH, D_MODEL), 0.01),
        "ln2_b": nrm(ks[17], (DEPTH, D_MODEL), 0.01),
        "ffn2_w_in": nrm(ks[18], (DEPTH, D_MODEL, 2 * D_FF), D_MODEL ** -0.5),
        "ffn2_w_out": nrm(ks[19], (DEPTH, D_FF, D_MODEL), DEEPNORM_BETA * D_FF ** -0.5),
        "ln3_g": 1.0 + nrm(ks[20], (DEPTH, D_MODEL), 0.01),
        "ln3_b": nrm(ks[21], (DEPTH, D_MODEL), 0.01),
    }


def reference(x_prompt, x_sample, cache_k, cache_v, ffn1_w_in, ffn1_w_out, ln1_g, ln1_b, w_in,
              sgu_w, sgu_b, sgu_v_g, sgu_v_b, out_a_g, out_b_g, w_out, ln2_g, ln2_b,
              ffn2_w_in, ffn2_w_out, ln3_g, ln3_b):
    params = (ffn1_w_in, ffn1_w_out, ln1_g, ln1_b, w_in, sgu_w, sgu_b, sgu_v_g, sgu_v_b,
              out_a_g, out_b_g, w_out, ln2_g, ln2_b, ffn2_w_in, ffn2_w_out, ln3_g, ln3_b)
    y_p, y_s = x_prompt, x_sample
    kp_list, vp_list, ks_list, vs_list, us_list = [], [], [], [], []
    for layer in range(DEPTH):
        p = tuple(a[layer] for a in params)
        y_p, kp, vp, _ = _layer(y_p, p, None, None)
        y_s, ksmp, vsmp, usmp = _layer(y_s, p, cache_k[layer], cache_v[layer])
        kp_list.append(kp)
        vp_list.append(vp)
        ks_list.append(ksmp)
        vs_list.append(vsmp)
        us_list.append(usmp)
    cache_k_prompt = jnp.stack(kp_list)
    cache_v_prompt = jnp.stack(vp_list)
    cache_k_sample = jnp.stack(ks_list)
    cache_v_sample = jnp.stack(vs_list)
    sgu_v_sample = jnp.stack(us_list)
    return (y_p, y_s, cache_k_prompt, cache_v_prompt, cache_k_sample, cache_v_sample, sgu_v_sample)
```

```python
import numpy as np
from contextlib import ExitStack
import concourse.bass as bass
import concourse.mybir as mybir
from concourse.bass_utils import run_bass_kernel_spmd

F32 = mybir.dt.float32
BF16 = mybir.dt.bfloat16
AF = mybir.ActivationFunctionType
ALU = mybir.AluOpType
AX = mybir.AxisListType

D = 1024
FF = 2816
NFF = 22
SEQ = 8192
NB = 16
WBUF = 2048
ALPHA = 2.0 ** 0.25
EPS = 1e-5
EPSP = EPS / (ALPHA * ALPHA)
C1 = 0.5 / ALPHA
NSLOT = 5
PROBE_STOP = 0


class Prog:
    ENG = ["pe", "act", "dve", "pool", "sp"]

    def __init__(s):
        s.ops = {e: [] for e in s.ENG}
        s.cnt = {e: 0 for e in ["pe", "act", "dve", "pool"]}
        s.known = {e: {} for e in s.ENG}
        s.buf = {}
        s.dcnt = {}

    def _deps(s, eng, reads, writes):
        need = {}

        def add(tok):
            if tok is None:
                return
            sem, val = tok
            if sem == eng and eng == "pe":
                return
            if need.get(sem, 0) < val:
                need[sem] = val
        for k in reads:
            b = s.buf.get(k)
            if b:
                add(b[0])
        for k in writes:
            b = s.buf.get(k)
            if b:
                add(b[0])
                for sem, val in b[1].items():
                    add((sem, val))
        waits = []
        kn = s.known[eng]
        for sem, val in need.items():
            if kn.get(sem, 0) < val:
                kn[sem] = val
                waits.append((sem, val))
        return waits

    def _commit(s, tok, reads, writes):
        for k in reads:
            b = s.buf.setdefault(k, [None, {}])
            if b[1].get(tok[0], 0) < tok[1]:
                b[1][tok[0]] = tok[1]
        for k in writes:
            s.buf[k] = [tok, {}]

    def op(s, eng, fn, reads=(), writes=()):
        if eng != "pe":
            extra_w = [("psr", k[1]) for k in reads if isinstance(k, tuple) and k[0] == "ps"]
            if extra_w:
                writes = list(writes) + extra_w
        waits = s._deps(eng, reads, writes)
        s.cnt[eng] += 1
        tok = (eng, s.cnt[eng])
        s.ops[eng].append((waits, fn, (eng, 1)))
        s._commit(tok, reads, writes)

    def dma(s, semkey, fn, reads=(), writes=(), eng="sp", extra=None):
        waits = s._deps(eng, reads, writes)
        if extra:
            kn = s.known[eng]
            for sem, val in extra.items():
                if kn.get(sem, 0) < val:
                    kn[sem] = val
                    waits.append((sem, val))
        s.dcnt[semkey] = s.dcnt.get(semkey, 0) + 16
        tok = (semkey, s.dcnt[semkey])
        s.ops[eng].append((waits, fn, (semkey, 16)))
        s._commit(tok, reads, writes)

    def alias(s, old_keys, new_keys):
        acc = {}
        for k in old_keys:
            b = s.buf.get(k)
            if not b:
                continue
            toks = list(b[1].items())
            if b[0] is not None:
                toks.append(b[0])
            for sem, val in toks:
                if acc.get(sem, 0) < val:
                    acc[sem] = val
        for k in new_keys:
            b = s.buf.setdefault(k, [None, {}])
            for sem, val in acc.items():
                if b[1].get(sem, 0) < val:
                    b[1][sem] = val


def build_nc(NT=16, with_sample=True, stage=99):
    nc = bass.Bass("TRN2", target_bir_lowering=False)
    P = Prog()
    dt_in = lambda n, s: nc.dram_tensor(n, s, F32, kind="ExternalInput").ap()
    dt_out = lambda n, s: nc.dram_tensor(n, s, F32, kind="ExternalOutput").ap()
    S = NT * 512
    xp = dt_in("xp", [S, D])
    xs = dt_in("xs", [64, D])
    ck = dt_in("ck", [NB, WBUF, 512]) if with_sample else None
    cv = dt_in("cv", [NB, WBUF, 512]) if with_sample else None
    w1 = [dt_in("ffn1_w_in", [D, 2 * FF]), dt_in("ffn2_w_in", [D, 2 * FF])]
    w2 = [dt_in("ffn1_w_out", [FF, D]), dt_in("ffn2_w_out", [FF, D])]
    w_in = dt_in("w_in", [D, 2560])
    w_out = dt_in("w_out", [D, D])
    lnp = dt_in("lnp", [6, D])
    sgu_w = dt_in("sgu_w", [4, 128, 128])
    sgu_b = dt_in("sgu_b", [4, 128])
    sgu_vgb = dt_in("sgu_vgb", [2, 512])
    out_g = dt_in("out_g", [2, 512])
    cmask = dt_in("cmask", [128, 384])
    cmask2 = dt_in("cmask2", [128, 256])
    yp = dt_out("yp", [S, D])
    ys = dt_out("ys", [64, D])
    kp = dt_out("kp", [2048, 512])
    vp = dt_out("vp", [2048, 512])
    cks = dt_out("cks", [NB, WBUF, 512]) if with_sample else None
    cvs = dt_out("cvs", [NB, WBUF, 512]) if with_sample else None
    sgo = dt_out("sgo", [64, 512])
    s_w1 = [nc.dram_tensor("s_w1_%d" % f, [NFF, 128, 2048], BF16).ap() for f in range(2)]
    s_w2 = [nc.dram_tensor("s_w2_%d" % f, [8, 128, 2816], BF16).ap() for f in range(2)]
    s_win = nc.dram_tensor("s_win", [10, 128, 2048], BF16).ap()
    s_wo = nc.dram_tensor("s_wo", [8, 128, 1024], BF16).ap()
    vsc = nc.dram_tensor("vsc", [S, 512], BF16).ap()

    sb = lambda n, s, d=F32: nc.alloc_sbuf_tensor(n, s, d)
    WR = sb("WR", [128, NSLOT, 2816], BF16)
    XT = sb("XT", [128, 8, 512])
    XB = sb("XB", [128, 8, 512], BF16)
    R = sb("R", [128, 11264], BF16)
    actT = R[:, :].rearrange("p (c t) -> p c t", c=NFF)
    QT = R[:, 0:2048].rearrange("p (c t) -> p c t", c=4)
    OA = R[:, 2048:4096].rearrange("p (c t) -> p c t", c=4)
    OBf = R[:, 4096:8192].bitcast(F32).rearrange("p (c t) -> p c t", c=4)
    OBb = R[:, 8192:10240].rearrange("p (c t) -> p c t", c=4)
    PT = R[:, 10240:11264].rearrange("p (c t) -> p c t", c=2)
    KT = sb("KT", [128, 4, 4096], BF16)
    VN = sb("VN", [128, 2, 4, 512], BF16)
    V4 = sb("V4", [128, 2, 4, 512], BF16)
    V16 = sb("V16", [128, 2, 16, 512], BF16)
    XIN = sb("XIN", [128, 3, 1024])
    YO = sb("YO", [128, 2, 1024])
    ZB = sb("ZB", [128, 2, 512], BF16)
    ZQ = sb("ZQ", [128, 2, 512], BF16)
    TMPA = sb("TMPA", [128, 2, 512])
    TMPB = sb("TMPB", [128, 2, 512])
    RA = sb("RA", [128, 512])
    RB = sb("RB", [128, 512])
    MEANS = RA
    RSTD = RB
    LNP = sb("LNP", [128, 6, 8])
    IDN = sb("IDN", [128, 128])
    MSKF = sb("MSKF", [128, 256])
    MLE = sb("MLE", [128, 128], BF16)
    MGE = sb("MGE", [128, 128], BF16)
    ONESM = sb("ONESM", [128, 128], BF16)
    ONESB = sb("ONESB", [128, 128], BF16)
    ONES1 = sb("ONES1", [128, 64], BF16)
    WST = sb("WST", [128, 4, 128], BF16)
    SBIAS = sb("SBIAS", [128, 512])
    VGB = sb("VGB", [128, 2, 512])
    OUTG = sb("OUTG", [128, 2, 4])
    VNB = sb("VNB", [128, 4, 512], BF16)
    SMALL = sb("SMALL", [128, 16])
    CM2F = sb("CM2F", [128, 256])
    CNTB = sb("CNTB", [128, 32], BF16)
    CNT7B = sb("CNT7B", [128, 64], BF16)
    ASEL = sb("ASEL", [128, 64], BF16)
    WB = sb("WB", [128, 4, 64], BF16)
    RS = sb("RS", [128, 4, 64], BF16)
    CSTG = KT[:, :, :].rearrange("p c t -> p (c t)")[:, 0:11264].bitcast(F32).rearrange("p (a n) -> p a n", a=2)
    CSTB = V16[:, :, :, :].rearrange("p a r c -> p (a r c)")[:, 0:5632].rearrange("p (a n) -> p a n", a=2)
    WSF = TMPB[:, :, :].rearrange("p a n -> p (a n)")[:, 0:512].rearrange("p (g j) -> p g j", g=4)
    PS = [nc.alloc_psum_tensor("ps%d" % i, [128, 512], F32) for i in range(8)]
    psk = lambda i: ("ps", i)

    def dsimple(semkey, out, in_, reads=(), writes=(), slow=False):
        if slow:
            P.dma(semkey, lambda e: e.dma_start(out=out, in_=in_, allow_slow_non_contiguous=True), reads, writes)
        else:
            P.dma(semkey, lambda e: e.dma_start(out=out, in_=in_), reads, writes)

    dsimple("par", IDN[:], cmask[:, 0:128], writes=["params"])
    dsimple("par", MSKF[:], cmask[:, 128:384], writes=["params"])
    dsimple("par", CM2F[:], cmask2[:, :], writes=["params"])
    dsimple("par", LNP[:], lnp.rearrange("k (c p) -> p k c", p=128), writes=["params"], slow=True)
    dsimple("par", WSF[:], sgu_w.rearrange("g i j -> i g j"), writes=["params"])
    dsimple("par", SBIAS[:], sgu_b.rearrange("g i -> (g i)").partition_broadcast(128), writes=["params"])
    dsimple("par", VGB[:].rearrange("p a b -> p (a b)"), sgu_vgb.rearrange("a b -> (a b)").partition_broadcast(128), writes=["params"])
    dsimple("par", OUTG[:], out_g.rearrange("k (c p) -> p k c", p=128), writes=["params"], slow=True)
    P.op("pool", lambda e: e.tensor_copy(out=MLE[:], in_=MSKF[:, 0:128]), reads=["params"], writes=["consts"])
    P.op("pool", lambda e: e.tensor_copy(out=MGE[:], in_=MSKF[:, 128:256]), reads=["params"], writes=["consts"])
    P.op("pool", lambda e: e.tensor_copy(out=CNTB[:], in_=CM2F[:, 0:32]), reads=["params"], writes=["consts"])
    P.op("pool", lambda e: e.tensor_copy(out=CNT7B[:], in_=CM2F[:, 128:192]), reads=["params"], writes=["consts"])
    P.op("pool", lambda e: e.tensor_copy(out=ASEL[:], in_=CM2F[:, 192:256]), reads=["params"], writes=["consts"])
    P.op("pool", lambda e: e.memset(ONESM[:], 1.0 / 1024), writes=["consts"])
    P.op("pool", lambda e: e.memset(ONESB[:], 1.0 / 512), writes=["consts"])
    P.op("pool", lambda e: e.memset(ONES1[:], 1.0), writes=["consts"])
    P.op("pool", lambda e: e.memset(VN[:], 0.0), writes=[("VN", i) for i in range(2)])
    P.op("pool", lambda e: e.memset(V4[:], 0.0), writes=[("V4", i) for i in range(2)])
    for g in range(4):
        P.op("pe", lambda e, g=g: e.transpose(out=PS[0][:, g * 128:(g + 1) * 128], in_=WSF[:, g, :], identity=IDN[:]),
             reads=["params"], writes=[psk(0)])
    P.op("dve", lambda e: e.tensor_tensor(out=WST[:], in0=PS[0][:].rearrange("p (g i) -> p g i", g=4),
                                          in1=MSKF[:, 0:128].unsqueeze(1).to_broadcast([128, 4, 128]), op=ALU.mult),
         reads=[psk(0), "params"], writes=["consts"])

    cjob = [0]

    scr_tok = {}

    def convert(src, dst, n, dkey, m=128):
        j = cjob[0]
        cjob[0] += 1
        sl = j % 2
        dsimple("cin%d" % sl, CSTG[:, sl, 0:n], src, writes=[("cstg", sl)])
        eng = ["dve", "pool", "act"][j % 3]
        if eng == "act":
            P.op("act", lambda e: e.copy(out=CSTB[:, sl, 0:n], in_=CSTG[:, sl, 0:n]), reads=[("cstg", sl)], writes=[("cstb", sl)])
        else:
            P.op(eng, lambda e: e.tensor_copy(out=CSTB[:, sl, 0:n], in_=CSTG[:, sl, 0:n]), reads=[("cstg", sl)], writes=[("cstb", sl)])
        P.dma("cout%d" % sl, lambda e: e.dma_start(out=dst, in_=CSTB[:, sl, 0:n].rearrange("p (a m) -> p a m", m=m)),
              reads=[("cstb", sl)], writes=[])
        scr_tok.setdefault(dkey, {})["cout%d" % sl] = P.dcnt["cout%d" % sl]

    def conv_all():
        for f in range(2):
            for kc in range(8):
                for half in range(2):
                    src = w1[f][kc * 128:(kc + 1) * 128, half * FF:(half + 1) * FF]
                    dst = s_w1[f].rearrange("c p (k h m) -> p c k h m", k=8, h=2)[:, :, kc, half, :]
                    convert(src, dst, FF, ("s_w1", f))
            for c in range(NFF):
                src = w2[f][c * 128:(c + 1) * 128, :]
                dst = s_w2[f].rearrange("d p (c m) -> p d c m", c=NFF)[:, :, c, :]
                convert(src, dst, D, ("s_w2", f))
            if f == 0:
                for kc in range(8):
                    src = w_in[kc * 128:(kc + 1) * 128, :]
                    dst = s_win.rearrange("j p (k m) -> p j k m", k=8)[:, :, kc, :]
                    convert(src, dst, 2560, "s_win", m=256)
                for c in range(8):
                    src = w_out[c * 128:(c + 1) * 128, :]
                    dst = s_wo.rearrange("d p (c m) -> p d c m", c=8)[:, :, c, :]
                    convert(src, dst, D, "s_wo")

    conv_all()
    P.alias([("cstg", 0), ("cstg", 1), ("cstb", 0), ("cstb", 1)], [("KT", i) for i in range(8)] + [("V16", i, q) for i in range(2) for q in range(4)])
    P.op("pool", lambda e: e.memset(KT[:], 0.0), writes=[("KT", i) for i in range(8)])
    P.op("pool", lambda e: e.memset(V16[:], 0.0), writes=[("V16", i, q) for i in range(2) for q in range(4)])

    units = []
    def tile_units():
        u = []
        for c in range(NFF):
            u.append((s_w1[0][c], 2048, ("s_w1", 0)))
        for d_ in range(8):
            u.append((s_w2[0][d_], 2816, ("s_w2", 0)))
        for j in [4, 5, 0, 1, 2, 3, 8, 9, 6, 7]:
            u.append((s_win[j], 2048, "s_win"))
        for d_ in range(8):
            u.append((s_wo[d_], 1024, "s_wo"))
        for c in range(NFF):
            u.append((s_w1[1][c], 2048, ("s_w1", 1)))
        for d_ in range(8):
            u.append((s_w2[1][d_], 2816, ("s_w2", 1)))
        return u
    ntiles_total = NT + (1 if with_sample else 0)
    for _ in range(ntiles_total):
        units.extend(tile_units())
    wstate = {"i": 0, "l": 0}

    def wnext():
        i = wstate["i"]
        wstate["i"] += 1
        while wstate["l"] < min(len(units), i + NSLOT - 1):
            l = wstate["l"]
            ap, n, key = units[l]
            slot = l % NSLOT
            P.dma("wr%d" % slot, lambda e, ap=ap, n=n, slot=slot: e.dma_start(out=WR[:, slot, 0:n], in_=ap),
                  reads=[], writes=[("wr", slot)], extra=scr_tok[key])
            wstate["l"] += 1
        return i % NSLOT

    def layer_norm(T, lg, lb, need_xb=True):
        cs = slice(0, T)
        P.op("act", lambda e: e.copy(out=MEANS[:, cs], in_=PS[6][:, cs]), reads=[psk(6)], writes=["RA"])
        P.op("dve", lambda e: e.tensor_tensor(out=TMPA[:, 0, cs], in0=MEANS[:, cs], in1=MEANS[:, cs], op=ALU.mult),
             reads=["RA"], writes=[("TMPA", 0)])
        P.op("dve", lambda e: e.scalar_tensor_tensor(out=TMPA[:, 0, cs], in0=PS[7][:, cs], scalar=EPSP, in1=TMPA[:, 0, cs],
                                                     op0=ALU.add, op1=ALU.subtract),
             reads=[psk(7), ("TMPA", 0)], writes=[("TMPA", 0)])
        P.op("act", lambda e: e.activation(out=RSTD[:, cs], in_=TMPA[:, 0, cs], func=AF.Sqrt), reads=[("TMPA", 0)], writes=["RB"])
        P.op("dve", lambda e: e.reciprocal(out=RSTD[:, cs], in_=RSTD[:, cs]), reads=["RB"], writes=["RB"])
        for dc in range(8):
            P.op("dve", lambda e, dc=dc: e.tensor_tensor(out=XT[:, dc, cs], in0=XT[:, dc, cs], in1=MEANS[:, cs], op=ALU.subtract),
                 reads=[("XT", dc), "RA"], writes=[("XT", dc)])
            P.op("pool", lambda e, dc=dc: e.tensor_tensor(out=XT[:, dc, cs], in0=XT[:, dc, cs], in1=RSTD[:, cs], op=ALU.mult),
                 reads=[("XT", dc), "RB"], writes=[("XT", dc)])
            P.op("act", lambda e, dc=dc: e.activation(out=XT[:, dc, cs], in_=XT[:, dc, cs], func=AF.Identity,
                                                      scale=LNP[:, lg, dc:dc + 1], bias=LNP[:, lb, dc:dc + 1]),
                 reads=[("XT", dc), "params"], writes=[("XT", dc)])
            if need_xb:
                P.op("pool", lambda e, dc=dc: e.tensor_copy(out=XB[:, dc, cs], in_=XT[:, dc, cs]),
                     reads=[("XT", dc)], writes=[("XB", dc)])

    def stats_accum(T, dc):
        cs = slice(0, T)
        sl = dc % 2
        P.op("pool", lambda e: e.tensor_copy(out=ZB[:, sl, cs], in_=XT[:, dc, cs]), reads=[("XT", dc)], writes=[("ZB", sl)])
        P.op("act", lambda e: e.activation(out=ZQ[:, sl, cs], in_=XT[:, dc, cs], func=AF.Square), reads=[("XT", dc)], writes=[("ZQ", sl)])
        P.op("pe", lambda e: e.matmul(PS[6][:, cs], ONESM[:], ZB[:, sl, cs], start=(dc == 0), stop=(dc == 7)),
             reads=[("ZB", sl), "consts"], writes=[psk(6)])
        P.op("pe", lambda e: e.matmul(PS[7][:, cs], ONESM[:], ZQ[:, sl, cs], start=(dc == 0), stop=(dc == 7)),
             reads=[("ZQ", sl), "consts"], writes=[psk(7)])

    def ffn(T, lg, lb, need_xb=True):
        cs = slice(0, T)
        actk = [("act", c) for c in range(NFF)]
        for c in range(NFF):
            slot = wnext()
            Wv = WR[:, slot, 0:2048].rearrange("p (k h m) -> p k h m", k=8, h=2)
            bg, bu = (0, 1) if c % 2 == 0 else (2, 3)
            for h, bank in ((0, bg), (1, bu)):
                for kc in range(8):
                    P.op("pe", lambda e, kc=kc, h=h, bank=bank, Wv=Wv: e.matmul(PS[bank][:, cs], Wv[:, kc, h, :], XB[:, kc, cs],
                                                                                  start=(kc == 0), stop=(kc == 7)),
                         reads=[("wr", slot), ("XB", kc)], writes=[psk(bank)])
            ts = c % 2
            P.op("act", lambda e, bg=bg, ts=ts: e.activation(out=TMPA[:, ts, cs], in_=PS[bg][:, cs], func=AF.Silu),
                 reads=[psk(bg)], writes=[("TMPA", ts)])
            P.op("dve", lambda e, bu=bu, ts=ts, c=c: e.tensor_tensor(out=actT[:, c, cs], in0=PS[bu][:, cs], in1=TMPA[:, ts, cs], op=ALU.mult),
                 reads=[psk(bu), ("TMPA", ts)], writes=[("act", c)])
        for dc in range(8):
            slot = wnext()
            Wv = WR[:, slot, 0:2816].rearrange("p (c m) -> p c m", c=NFF)
            bank = 4 + dc % 2
            for c in range(NFF):
                P.op("pe", lambda e, c=c, bank=bank, Wv=Wv: e.matmul(PS[bank][:, cs], Wv[:, c, :], actT[:, c, cs],
                                                                       start=(c == 0), stop=(c == NFF - 1)),
                     reads=[("wr", slot), ("act", c)], writes=[psk(bank)])
            P.op("dve", lambda e, dc=dc, bank=bank: e.scalar_tensor_tensor(out=XT[:, dc, cs], in0=PS[bank][:, cs], scalar=C1,
                                                                            in1=XT[:, dc, cs], op0=ALU.mult, op1=ALU.add),
                 reads=[psk(bank), ("XT", dc)], writes=[("XT", dc)])
            stats_accum(T, dc)
        layer_norm(T, lg, lb, need_xb)

    RKEYS_ATT = ["QT", "OA", "OBf", "OBb", ("PT", 0), ("PT", 1)]
    RKEYS_ACT = [("act", c) for c in range(NFF)]

    def mask_ap(M, lo, n, rep):
        return M[:, lo:lo + n].unsqueeze(2).to_broadcast([128, n, rep])

    def prompt_tile(ti):
        T = 512
        t0 = ti * 512
        ts8 = ti % 8
        kbase = ts8 * 512
        span = ti // 4
        qtr = ti % 4
        sbase = (span % 2) * 2048
        pbase = ((span - 1) % 2) * 2048
        vsl = ti % 2
        def xload(tj, s_):
            xs_ = (tj * 4 + s_) % 3
            r0_ = tj * 512 + s_ * 128
            dsimple("xin%d" % xs_, XIN[:, xs_, :], xp[r0_:r0_ + 128, :], writes=[("xin", xs_)])
        if ti == 0:
            for s_ in range(3):
                xload(0, s_)
        for s_ in range(4):
            xs_ = (ti * 4 + s_) % 3
            for hb in range(2):
                bank = (s_ * 2 + hb) % 4
                for d4 in range(4):
                    dc = hb * 4 + d4
                    P.op("pe", lambda e, dc=dc, d4=d4, bank=bank, xs_=xs_: e.transpose(out=PS[bank][:, d4 * 128:(d4 + 1) * 128],
                                                                               in_=XIN[:, xs_, dc * 128:(dc + 1) * 128], identity=IDN[:]),
                         reads=[("xin", xs_), "params"], writes=[psk(bank)])
                dks = [("XT", hb * 4 + d4) for d4 in range(4)]
                bks = [("XB", hb * 4 + d4) for d4 in range(4)]
                P.op("act", lambda e, hb=hb, bank=bank, s_=s_: e.copy(out=XT[:, hb * 4:hb * 4 + 4, s_ * 128:(s_ + 1) * 128],
                                                                      in_=PS[bank][:].rearrange("p (d t) -> p d t", d=4)),
                     reads=[psk(bank)], writes=dks)
                P.op("dve", lambda e, hb=hb, bank=bank, s_=s_: e.tensor_copy(out=XB[:, hb * 4:hb * 4 + 4, s_ * 128:(s_ + 1) * 128],
                                                                             in_=PS[bank][:].rearrange("p (d t) -> p d t", d=4)),
                     reads=[psk(bank)], writes=bks)
            if s_ == 0:
                xload(ti, 3)
        if ti + 1 < NT:
            for s_ in range(3):
                xload(ti + 1, s_)
        if stage < 2:
            return
        if with_sample and ti < NB:
            for (src_, dst_) in ((ck, cks), (cv, cvs)):
                P.dma("cc", lambda e, src_=src_, dst_=dst_: e.dma_start(
                    out=dst_[ti, 0:WBUF - 4, :].rearrange("r c -> (r c)").rearrange("(p n) -> p n", p=128),
                    in_=src_[ti, 4:WBUF, :].rearrange("r c -> (r c)").rearrange("(p n) -> p n", p=128)))
        P.alias(RKEYS_ATT, RKEYS_ACT)
        ffn(T, 0, 1)
        P.alias(RKEYS_ACT, RKEYS_ATT)
        if stage < 3:
            return
        last4 = ti >= NT - 4
        for j in (4, 5):
            slot = wnext()
            Wv = WR[:, slot, 0:2048].rearrange("p (k m) -> p k m", k=8)
            co = (j - 4) * 256
            for sp_ in range(2):
                bank = sp_
                for s2 in range(2):
                    s_ = sp_ * 2 + s2
                    for kc in range(8):
                        P.op("pe", lambda e, kc=kc, s_=s_, s2=s2, bank=bank, Wv=Wv: e.matmul(
                            PS[bank][:, s2 * 256:(s2 + 1) * 256], XB[:, kc, s_ * 128:(s_ + 1) * 128], Wv[:, kc, :],
                            start=(kc == 0), stop=(kc == 7)),
                            reads=[("wr", slot), ("XB", kc)], writes=[psk(bank)])
                P.op("dve", lambda e, sp_=sp_, bank=bank, co=co: e.tensor_copy(
                    out=VN[:, vsl, sp_ * 2:sp_ * 2 + 2, co:co + 256], in_=PS[bank][:].rearrange("p (s m) -> p s m", s=2)),
                    reads=[psk(bank)], writes=[("VN", vsl)])
                if last4:
                    ysl = sp_
                    P.op("act", lambda e, bank=bank, ysl=ysl: e.copy(out=YO[:, ysl, 0:512], in_=PS[bank][:]),
                         reads=[psk(bank)], writes=[("yo", ysl)])
                    r0 = t0 - (S - 2048) + sp_ * 256
                    P.dma("yo%d" % ysl, lambda e, ysl=ysl, r0=r0, co=co: e.dma_start(
                        out=vp[r0:r0 + 256, co:co + 256].rearrange("(s p) m -> p s m", p=128),
                        in_=YO[:, ysl, 0:512].rearrange("p (s m) -> p s m", s=2)),
                        reads=[("yo", ysl)], writes=[])
        P.dma("vst", lambda e: e.dma_start(out=vsc[t0:t0 + 512, :].rearrange("(s p) c -> p s c", p=128), in_=VN[:, vsl, :, :]),
              reads=[("VN", vsl)], writes=[("vsc", ti)])
        P.dma("v4_%d" % vsl, lambda e: e.dma_start(out=V4[:, vsl, :, :].rearrange("p r c -> p (r c)"),
                                                   in_=vsc[t0:t0 + 512, :].rearrange("(j r) c -> j (r c)", r=4)),
              reads=[("vsc", ti)], writes=[("V4", vsl)])
        ssl = span % 2
        P.dma("v16_%d_%d" % (ssl, qtr), lambda e: e.dma_start(
            out=V16[32 * qtr:32 * qtr + 32, ssl, :, :].rearrange("p r c -> p (r c)"),
            in_=vsc[t0:t0 + 512, :].rearrange("(i r) c -> i (r c)", r=16)),
            reads=[("vsc", ti)], writes=[("V16", ssl, qtr)])
        for j in (0, 1, 2, 3):
            slot = wnext()
            Wv = WR[:, slot, 0:2048].rearrange("p (k m) -> p k m", k=8)
            for o2 in range(2):
                bank = 2 + o2
                for kc in range(8):
                    P.op("pe", lambda e, kc=kc, o2=o2, bank=bank, Wv=Wv: e.matmul(
                        PS[bank][:], Wv[:, kc, o2 * 128:(o2 + 1) * 128], XB[:, kc, :], start=(kc == 0), stop=(kc == 7)),
                        reads=[("wr", slot), ("XB", kc)], writes=[psk(bank)])
                if j < 2:
                    oc = j * 2 + o2
                    P.op("act", lambda e, oc=oc, bank=bank: e.activation(out=QT[:, oc, :], in_=PS[bank][:], func=AF.Copy, scale=0.125),
                         reads=[psk(bank)], writes=["QT"])
                else:
                    oc = (j - 2) * 2 + o2
                    P.op("act", lambda e, oc=oc, bank=bank: e.copy(out=KT[:, oc, kbase:kbase + 512], in_=PS[bank][:]),
                         reads=[psk(bank)], writes=[("KT", ts8)])
            if j >= 2 and last4:
                co = (j - 2) * 256
                for sp_ in range(2):
                    bank = sp_
                    for s2 in range(2):
                        s_ = sp_ * 2 + s2
                        for kc in range(8):
                            P.op("pe", lambda e, kc=kc, s_=s_, s2=s2, bank=bank, Wv=Wv: e.matmul(
                                PS[bank][:, s2 * 256:(s2 + 1) * 256], XB[:, kc, s_ * 128:(s_ + 1) * 128], Wv[:, kc, :],
                                start=(kc == 0), stop=(kc == 7)),
                                reads=[("wr", slot), ("XB", kc)], writes=[psk(bank)])
                    ysl = sp_
                    P.op("act", lambda e, bank=bank, ysl=ysl: e.copy(out=YO[:, ysl, 0:512], in_=PS[bank][:]),
                         reads=[psk(bank)], writes=[("yo", ysl)])
                    r0 = t0 - (S - 2048) + sp_ * 256
                    P.dma("yo%d" % ysl, lambda e, ysl=ysl, r0=r0, co=co: e.dma_start(
                        out=kp[r0:r0 + 256, co:co + 256].rearrange("(s p) m -> p s m", p=128),
                        in_=YO[:, ysl, 0:512].rearrange("p (s m) -> p s m", s=2)),
                        reads=[("yo", ysl)], writes=[])
        if stage < 4:
            return
        sl8 = wnext()
        sl9 = wnext()
        W8 = WR[:, sl8, 0:2048].rearrange("p (k m) -> p k m", k=8)
        W9 = WR[:, sl9, 0:2048].rearrange("p (k m) -> p k m", k=8)
        for s_ in range(4):
            bank = s_ % 2
            for (Wv, slot, co) in ((W8, sl8, 0), (W9, sl9, 256)):
                for kc in range(8):
                    P.op("pe", lambda e, kc=kc, s_=s_, bank=bank, Wv=Wv, co=co: e.matmul(
                        PS[bank][:, co:co + 256], XB[:, kc, s_ * 128:(s_ + 1) * 128], Wv[:, kc, :], start=(kc == 0), stop=(kc == 7)),
                        reads=[("wr", slot), ("XB", kc)], writes=[psk(bank)])
            sgu_ln(bank, s_, 128)
        for g in range(4):
            bank = 2 + g % 2
            for s_ in range(4):
                P.op("pe", lambda e, g=g, s_=s_, bank=bank: e.matmul(PS[bank][:, s_ * 128:(s_ + 1) * 128], VNB[:, s_, g * 128:(g + 1) * 128],
                                                                      WST[:, g, :], start=True, stop=True),
                     reads=[("VNB", s_), "consts"], writes=[psk(bank)])
            sgu_gate(g, bank, T, 128)
        sgu_rms(T)
        if stage < 5:
            return
        for c in range(4):
            attention_pair(ti, c)
        P.op("dve", lambda e: e.tensor_scalar(out=TMPB[:, 1, :], in0=PS[3][:], scalar1=EPS, scalar2=None, op0=ALU.add),
             reads=[psk(3)], writes=[("TMPB", 1)])
        P.op("act", lambda e: e.activation(out=RA[:], in_=TMPB[:, 1, :], func=AF.Sqrt), reads=[("TMPB", 1)], writes=["RA"])
        P.op("dve", lambda e: e.reciprocal(out=RA[:], in_=RA[:]), reads=["RA"], writes=["RA"])
        if stage < 6:
            return
        wout_ln(T)
        if stage < 7:
            return
        P.alias(RKEYS_ATT, RKEYS_ACT)
        ffn(T, 4, 5, need_xb=False)
        for s_ in range(4):
            ysl = s_ % 2
            for hb in range(2):
                bank = hb
                for d4 in range(4):
                    dc = hb * 4 + d4
                    P.op("pe", lambda e, dc=dc, d4=d4, s_=s_, bank=bank: e.transpose(
                        out=PS[bank][:, d4 * 128:(d4 + 1) * 128], in_=XT[:, dc, s_ * 128:(s_ + 1) * 128], identity=IDN[:]),
                        reads=[("XT", dc), "params"], writes=[psk(bank)])
                if hb == 0:
                    P.op("act", lambda e, bank=bank, ysl=ysl: e.copy(out=YO[:, ysl, 0:512], in_=PS[bank][:]),
                         reads=[psk(bank)], writes=[("yo", ysl)])
                else:
                    P.op("dve", lambda e, bank=bank, ysl=ysl: e.tensor_copy(out=YO[:, ysl, 512:1024], in_=PS[bank][:]),
                         reads=[psk(bank)], writes=[("yo", ysl)])
            P.dma("yo%d" % ysl, lambda e, ysl=ysl, s_=s_: e.dma_start(out=yp[t0 + s_ * 128:t0 + (s_ + 1) * 128, :], in_=YO[:, ysl, :]),
                  reads=[("yo", ysl)], writes=[])

    def sgu_ln(bank, s_, npart):
        pp = slice(0, npart)
        P.op("act", lambda e: e.activation(out=TMPA[pp, 1, :], in_=PS[bank][pp, :], func=AF.Identity, accum_out=SMALL[pp, 0:1]),
             reads=[psk(bank)], writes=[("TMPA", 1), "SMALL"])
        P.op("act", lambda e: e.activation(out=TMPA[pp, 1, :], in_=PS[bank][pp, :], func=AF.Square, accum_out=SMALL[pp, 1:2]),
             reads=[psk(bank)], writes=[("TMPA", 1), "SMALL"])
        P.op("dve", lambda e: e.tensor_scalar(out=SMALL[pp, 2:3], in0=SMALL[pp, 0:1], scalar1=1.0 / 512, scalar2=None, op0=ALU.mult),
             reads=["SMALL"], writes=["SMALL"])
        P.op("dve", lambda e: e.tensor_tensor(out=SMALL[pp, 3:4], in0=SMALL[pp, 2:3], in1=SMALL[pp, 2:3], op=ALU.mult),
             reads=["SMALL"], writes=["SMALL"])
        P.op("dve", lambda e: e.scalar_tensor_tensor(out=SMALL[pp, 4:5], in0=SMALL[pp, 1:2], scalar=1.0 / 512, in1=SMALL[pp, 3:4],
                                                     op0=ALU.mult, op1=ALU.subtract), reads=["SMALL"], writes=["SMALL"])
        P.op("dve", lambda e: e.tensor_scalar(out=SMALL[pp, 4:5], in0=SMALL[pp, 4:5], scalar1=EPS, scalar2=None, op0=ALU.add),
             reads=["SMALL"], writes=["SMALL"])
        P.op("act", lambda e: e.activation(out=SMALL[pp, 5:6], in_=SMALL[pp, 4:5], func=AF.Sqrt), reads=["SMALL"], writes=["SMALL"])
        P.op("dve", lambda e: e.reciprocal(out=SMALL[pp, 6:7], in_=SMALL[pp, 5:6]), reads=["SMALL"], writes=["SMALL"])
        P.op("dve", lambda e: e.tensor_scalar(out=TMPB[pp, 0, :], in0=PS[bank][pp, :], scalar1=SMALL[pp, 2:3], scalar2=SMALL[pp, 6:7],
                                              op0=ALU.subtract, op1=ALU.mult), reads=[psk(bank), "SMALL"], writes=[("TMPB", 0)])
        P.op("pool", lambda e: e.tensor_tensor(out=TMPB[pp, 0, :], in0=TMPB[pp, 0, :], in1=VGB[pp, 0, :], op=ALU.mult),
             reads=[("TMPB", 0), "params"], writes=[("TMPB", 0)])
        P.op("pool", lambda e: e.tensor_tensor(out=TMPB[pp, 0, :], in0=TMPB[pp, 0, :], in1=VGB[pp, 1, :], op=ALU.add),
             reads=[("TMPB", 0), "params"], writes=[("TMPB", 0)])
        P.op("pool", lambda e: e.tensor_copy(out=VNB[pp, s_, :], in_=TMPB[pp, 0, :]), reads=[("TMPB", 0)], writes=[("VNB", s_)])

    def sgu_gate(g, bank, T, chunk, sample=False):
        cs = slice(0, T)
        nrep = T // chunk
        if sample:
            bias_ap = SBIAS[:, g * 128:g * 128 + 4].unsqueeze(2).to_broadcast([128, 4, 16])
            P.op("dve", lambda e: e.tensor_tensor(out=TMPA[:, 0, cs].rearrange("p (t b) -> p t b", b=16),
                                                  in0=PS[bank][:, cs].rearrange("p (t b) -> p t b", b=16), in1=bias_ap, op=ALU.add),
                 reads=[psk(bank), "params"], writes=[("TMPA", 0)])
        else:
            bias_ap = SBIAS[:, g * 128:g * 128 + chunk].unsqueeze(1).to_broadcast([128, nrep, chunk])
            P.op("dve", lambda e: e.tensor_tensor(out=TMPA[:, 0, cs].rearrange("p (s i) -> p s i", i=chunk),
                                                  in0=PS[bank][:, cs].rearrange("p (s i) -> p s i", i=chunk), in1=bias_ap, op=ALU.add),
                 reads=[psk(bank), "params"], writes=[("TMPA", 0)])
        if g % 2 == 0:
            sgu_gate.slot = wnext()
        slot = sgu_gate.slot
        Wv = WR[:, slot, 0:2048].rearrange("p (k m) -> p k m", k=8)
        ub = 4 + g % 2
        for kc in range(8):
            P.op("pe", lambda e, kc=kc: e.matmul(PS[ub][:, cs], Wv[:, kc, (g % 2) * 128:(g % 2 + 1) * 128], XB[:, kc, cs],
                                                   start=(kc == 0), stop=(kc == 7)),
                 reads=[("wr", slot), ("XB", kc)], writes=[psk(ub)])
        P.op("dve", lambda e: e.tensor_tensor(out=OBf[:, g, cs], in0=PS[ub][:, cs], in1=TMPA[:, 0, cs], op=ALU.mult),
             reads=[psk(ub), ("TMPA", 0)], writes=["OBf"])
        P.op("act", lambda e: e.activation(out=ZQ[:, g % 2, cs], in_=OBf[:, g, cs], func=AF.Square), reads=["OBf"], writes=[("ZQ", g % 2)])
        P.op("pe", lambda e: e.matmul(PS[6][:, cs], ONESB[:], ZQ[:, g % 2, cs], start=(g == 0), stop=(g == 3)),
             reads=[("ZQ", g % 2), "consts"], writes=[psk(6)])
        P.op("act", lambda e: e.activation(out=OBb[:, g, cs], in_=OBf[:, g, cs], func=AF.Identity, scale=OUTG[:, 1, g:g + 1]),
             reads=["OBf", "params"], writes=["OBb"])

    def sgu_rms(T):
        cs = slice(0, T)
        P.op("dve", lambda e: e.tensor_scalar(out=TMPB[:, 1, cs], in0=PS[6][:, cs], scalar1=EPS, scalar2=None, op0=ALU.add),
             reads=[psk(6)], writes=[("TMPB", 1)])
        P.op("act", lambda e: e.activation(out=RB[:, cs], in_=TMPB[:, 1, cs], func=AF.Sqrt), reads=[("TMPB", 1)], writes=["RB"])
        P.op("dve", lambda e: e.reciprocal(out=RB[:, cs], in_=RB[:, cs]), reads=["RB"], writes=["RB"])

    def attention_pair(ti, c):
        ts8 = ti % 8
        kbase = ts8 * 512
        pts8 = (ti - 1) % 8
        span = ti // 4
        qtr = ti % 4
        sbase = (span % 2) * 2048
        pbase = ((span - 1) % 2) * 2048
        vsl = ti % 2
        pvsl = (ti - 1) % 2
        ssl = span % 2
        pssl = (span - 1) % 2
        NUM, DEN = 4 + c % 2, 6 + c % 2
        contribs = [("1c", True), ("1p", True), ("4c", True), ("4p", ti > 0), ("16c", True), ("16p", span > 0)]
        contribs = [n for n, ex in contribs if ex]
        items = [(hh, ci) for hh in range(2) for ci in range(len(contribs))]

        def scores(n):
            hh, ci = items[n]
            name = contribs[ci]
            po = hh * 64
            pr = slice(po, po + 64)
            sb_ = n % 3
            ptb = n % 2
            c0 = 0
            mm = []
            if name == "1c":
                for b_ in range(4):
                    mm.append((slice(b_ * 128, (b_ + 1) * 128), KT[pr, c, kbase + b_ * 128:kbase + (b_ + 1) * 128]))
                kreads = [("KT", ts8)]
            elif name == "1p":
                for b_ in range(4):
                    if b_ == 0:
                        if ti == 0:
                            continue
                        kb = pts8 * 512 + 384
                    else:
                        kb = kbase + (b_ - 1) * 128
                    mm.append((slice(b_ * 128, (b_ + 1) * 128), KT[pr, c, kb:kb + 128]))
                kreads = [("KT", ts8), ("KT", pts8)]
                if ti == 0:
                    c0 = 128
            elif name in ("4c", "4p"):
                base = kbase if name == "4c" else pts8 * 512
                for r in range(4):
                    mm.append((slice(r, 512, 4), KT[pr, c, base + r:base + 512:4]))
                kreads = [("KT", ts8 if name == "4c" else pts8)]
            else:
                base = sbase if name == "16c" else pbase
                for r in range(16):
                    mm.append((slice(r, 512, 16), KT[pr, c, base + r:base + 2048:16]))
                kreads = [("KT", i) for i in range(8)]
            for (osl, lhsT) in mm:
                P.op("pe", lambda e, osl=osl, lhsT=lhsT, sb_=sb_, pr=pr: e.matmul(PS[sb_][:, osl], lhsT, QT[pr, c, osl], start=True, stop=True),
                     reads=kreads + ["QT"], writes=[psk(sb_)])
            cs = slice(c0, 512)
            P.op("act", lambda e, sb_=sb_, ptb=ptb, cs=cs: e.activation(out=PT[:, ptb, cs], in_=PS[sb_][:, cs], func=AF.Exp),
                 reads=[psk(sb_)], writes=[("PT", ptb)])
            if name in ("1c", "1p"):
                M = MLE if name == "1c" else MGE
                nb_ = (512 - c0) // 128
                in1 = M[:, :].unsqueeze(1).to_broadcast([128, nb_, 128])
                view = PT[:, ptb, cs].rearrange("p (b q) -> p b q", q=128)
            elif name in ("4c", "4p"):
                M = MLE if name == "4c" else MGE
                in1 = mask_ap(M, 0, 128, 4)
                view = PT[:, ptb, :].rearrange("p (j r) -> p j r", r=4)
            else:
                M = MLE if name == "16c" else MGE
                in1 = mask_ap(M, 32 * qtr, 32, 16)
                view = PT[:, ptb, :].rearrange("p (i r) -> p i r", r=16)
            P.op("pool", lambda e, view=view, in1=in1: e.tensor_tensor(out=view, in0=view, in1=in1, op=ALU.mult),
                 reads=[("PT", ptb), "consts"], writes=[("PT", ptb)])

        def pvs(n):
            hh, ci = items[n]
            name = contribs[ci]
            h = 2 * c + hh
            po = hh * 64
            pr = slice(po, po + 64)
            tp = (0, 64) if hh == 1 else None
            ptb = n % 2
            lastc = (ci == len(contribs) - 1)
            c0 = 128 if (name == "1p" and ti == 0) else 0
            cs = slice(c0, 512)
            pv = []
            if name == "1c":
                for b_ in range(4):
                    pv.append((slice(b_ * 128, (b_ + 1) * 128), VN[:, vsl, b_, h * 64:(h + 1) * 64]))
                vreads = [("VN", vsl)]
            elif name == "1p":
                for b_ in range(4):
                    if b_ == 0:
                        if ti == 0:
                            continue
                        pv.append((slice(0, 128), VN[:, pvsl, 3, h * 64:(h + 1) * 64]))
                    else:
                        pv.append((slice(b_ * 128, (b_ + 1) * 128), VN[:, vsl, b_ - 1, h * 64:(h + 1) * 64]))
                vreads = [("VN", vsl), ("VN", pvsl)]
            elif name in ("4c", "4p"):
                sl_ = vsl if name == "4c" else pvsl
                for r in range(4):
                    pv.append((slice(r, 512, 4), V4[:, sl_, r, h * 64:(h + 1) * 64]))
                vreads = [("V4", sl_)]
            else:
                sl_ = ssl if name == "16c" else pssl
                for r in range(16):
                    pv.append((slice(r, 512, 16), V16[:, sl_, r, h * 64:(h + 1) * 64]))
                vreads = [("V16", sl_, q) for q in range(4)]
            first = (name == "1c")
            for pi_, (osl, vap) in enumerate(pv):
                st_ = first and pi_ == 0
                P.op("pe", lambda e, osl=osl, vap=vap, ptb=ptb, st_=st_, pr=pr, tp=tp, lastc=lastc: e.matmul(
                    PS[NUM][pr, osl], vap, PT[:, ptb, osl], start=st_, stop=lastc, tile_position=tp, skip_group_check=True),
                    reads=vreads + [("PT", ptb)], writes=[psk(NUM)])
            P.op("pe", lambda e, ptb=ptb, cs=cs, first=first, pr=pr, tp=tp, lastc=lastc: e.matmul(
                PS[DEN][pr, cs], ONES1[:, :], PT[:, ptb, cs], start=first, stop=lastc, tile_position=tp, skip_group_check=True),
                reads=[("PT", ptb), "consts"], writes=[psk(DEN)])

        scores(0)
        for n in range(len(items)):
            if n + 1 < len(items):
                scores(n + 1)
            pvs(n)
        P.op("dve", lambda e: e.reciprocal(out=TMPA[:, 1, :], in_=PS[DEN][:]), reads=[psk(DEN)], writes=[("TMPA", 1)])
        P.op("dve", lambda e: e.tensor_tensor(out=TMPB[:, 0, :], in0=PS[NUM][:], in1=TMPA[:, 1, :], op=ALU.mult),
             reads=[psk(NUM), ("TMPA", 1)], writes=[("TMPB", 0)])
        P.op("act", lambda e: e.activation(out=ZQ[:, c % 2, :], in_=TMPB[:, 0, :], func=AF.Square), reads=[("TMPB", 0)], writes=[("ZQ", c % 2)])
        P.op("pe", lambda e: e.matmul(PS[3][:], ONESB[:], ZQ[:, c % 2, :], start=(c == 0), stop=(c == 3)),
             reads=[("ZQ", c % 2), "consts"], writes=[psk(3)])
        P.op("act", lambda e: e.activation(out=OA[:, c, :], in_=TMPB[:, 0, :], func=AF.Identity, scale=OUTG[:, 0, c:c + 1]),
             reads=[("TMPB", 0), "params"], writes=["OA"])

    def wout_ln(T):
        cs = slice(0, T)
        for dc in range(8):
            slot = wnext()
            Wv = WR[:, slot, 0:1024].rearrange("p (c m) -> p c m", c=8)
            ba, bb = (0, 1) if dc % 2 == 0 else (2, 3)
            for c in range(4):
                P.op("pe", lambda e, c=c, ba=ba, Wv=Wv: e.matmul(PS[ba][:, cs], Wv[:, c, :], OA[:, c, cs], start=(c == 0), stop=(c == 3)),
                     reads=[("wr", slot), "OA"], writes=[psk(ba)])
            for c in range(4):
                P.op("pe", lambda e, c=c, bb=bb, Wv=Wv: e.matmul(PS[bb][:, cs], Wv[:, 4 + c, :], OBb[:, c, cs], start=(c == 0), stop=(c == 3)),
                     reads=[("wr", slot), "OBb"], writes=[psk(bb)])
            P.op("dve", lambda e, ba=ba: e.tensor_tensor(out=TMPA[:, 0, cs], in0=PS[ba][:, cs], in1=RA[:, cs], op=ALU.mult),
                 reads=[psk(ba), "RA"], writes=[("TMPA", 0)])
            P.op("dve", lambda e, bb=bb: e.tensor_tensor(out=TMPA[:, 1, cs], in0=PS[bb][:, cs], in1=RB[:, cs], op=ALU.mult),
                 reads=[psk(bb), "RB"], writes=[("TMPA", 1)])
            P.op("pool", lambda e: e.tensor_tensor(out=TMPA[:, 0, cs], in0=TMPA[:, 0, cs], in1=TMPA[:, 1, cs], op=ALU.add),
                 reads=[("TMPA", 0), ("TMPA", 1)], writes=[("TMPA", 0)])
            P.op("dve", lambda e, dc=dc: e.scalar_tensor_tensor(out=XT[:, dc, cs], in0=TMPA[:, 0, cs], scalar=1.0 / ALPHA, in1=XT[:, dc, cs],
                                                                 op0=ALU.mult, op1=ALU.add),
                 reads=[("TMPA", 0), ("XT", dc)], writes=[("XT", dc)])
            stats_accum(T, dc)
        layer_norm(T, 2, 3)


    def ra_from_ps3(T):
        cs = slice(0, T)
        P.op("dve", lambda e: e.tensor_scalar(out=TMPB[:, 1, cs], in0=PS[3][:, cs], scalar1=EPS, scalar2=None, op0=ALU.add),
             reads=[psk(3)], writes=[("TMPB", 1)])
        P.op("act", lambda e: e.activation(out=RA[:, cs], in_=TMPB[:, 1, cs], func=AF.Sqrt), reads=[("TMPB", 1)], writes=["RA"])
        P.op("dve", lambda e: e.reciprocal(out=RA[:, cs], in_=RA[:, cs]), reads=["RA"], writes=["RA"])

    def sample_tile():
        T = 64
        cs = slice(0, 64)
        KTf = KT[:, :, :].rearrange("p c t -> p (c t)").bitcast(F32)
        KSTG = KTf[:, 0:7168].rearrange("p (a t n) -> p a t n", a=2, t=7)
        V16f = V16[:, :, :, :].rearrange("p a r c -> p (a r c)").bitcast(F32)
        VSTG = V16f[:, 0:7168].rearrange("p (a t n) -> p a t n", a=2, t=7)
        VNEW = V16f[:, 7168:7680]
        KTs = VN[:, :, :, :].rearrange("p a s c -> p (a s c)").rearrange("p (c r) -> p c r", c=4)
        Vb = V4[:, :, :, :].rearrange("p a r c -> p (a r c)").rearrange("p (t n) -> p t n", t=8)
        MD = CM2F[0:16, 96:100]
        BD = CM2F[0:64, 32:96]
        old = ([("KT", i) for i in range(8)] + [("V16", i, q) for i in range(2) for q in range(4)]
               + [("VN", i) for i in range(2)] + [("V4", i) for i in range(2)])
        kkeys = [[("kstg", a, i) for i in range(5)] for a in range(2)]
        vkeys = [[("vstg", a, i) for i in range(5)] for a in range(2)]
        P.alias(old, kkeys[0] + kkeys[1] + vkeys[0] + vkeys[1] + ["vnew", "KTs", "Vb"])
        xs_v = xs.rearrange("(b t) d -> t b d", t=4)
        ys_v = ys.rearrange("(b t) d -> t b d", t=4)
        sgo_v = sgo.rearrange("(b t) d -> t b d", t=4)
        for t in range(4):
            dsimple("xin0", XIN[t * 16:(t + 1) * 16, 0, :], xs_v[t], writes=[("xin", 0, t)])
        for hb in range(2):
            bank = hb
            for d4 in range(4):
                dc = hb * 4 + d4
                P.op("pe", lambda e, dc=dc, d4=d4, bank=bank: e.transpose(out=PS[bank][:, d4 * 64:(d4 + 1) * 64],
                                                                           in_=XIN[0:64, 0, dc * 128:(dc + 1) * 128], identity=IDN[0:64, 0:64]),
                     reads=[("xin", 0), "params"] + [("xin", 0, t) for t in range(4)], writes=[psk(bank)])
            P.op("act", lambda e, hb=hb, bank=bank: e.copy(out=XT[:, hb * 4:hb * 4 + 4, 0:64],
                                                           in_=PS[bank][:, 0:256].rearrange("p (d t) -> p d t", d=4)),
                 reads=[psk(bank)], writes=[("XT", hb * 4 + d4) for d4 in range(4)])
            P.op("dve", lambda e, hb=hb, bank=bank: e.tensor_copy(out=XB[:, hb * 4:hb * 4 + 4, 0:64],
                                                                  in_=PS[bank][:, 0:256].rearrange("p (d t) -> p d t", d=4)),
                 reads=[psk(bank)], writes=[("XB", hb * 4 + d4) for d4 in range(4)])
        if PROBE_STOP == 1:
            return
        P.op("dve", lambda e: e.tensor_copy(out=WB[0:4, :, :].rearrange("p g (i b) -> p g i b", b=16),
                                            in_=WST[0:4, :, 0:4].unsqueeze(3).to_broadcast([4, 4, 4, 16])),
             reads=["consts"], writes=["WB"])
        for g in range(4):
            P.op("pe", lambda e, g=g: e.matmul(PS[2][0:64, g * 64:(g + 1) * 64], ASEL[0:4, 0:64], WB[0:4, g, :], start=True, stop=True),
                 reads=["WB", "consts"], writes=[psk(2)])
        P.op("dve", lambda e: e.tensor_tensor(
            out=RS[0:64, :, :], in0=PS[2][0:64, 0:256].rearrange("p (g x) -> p g x", g=4),
            in1=BD.unsqueeze(1).to_broadcast([64, 4, 64]), op=ALU.mult),
            reads=[psk(2), "params"], writes=["RS"])
        P.alias(RKEYS_ATT, RKEYS_ACT)
        ffn(T, 0, 1)
        P.alias(RKEYS_ACT, RKEYS_ATT)
        for j in (4, 5):
            slot = wnext()
            Wv = WR[:, slot, 0:2048].rearrange("p (k m) -> p k m", k=8)
            co = (j - 4) * 256
            for kc in range(8):
                P.op("pe", lambda e, kc=kc, Wv=Wv, co=co: e.matmul(PS[0][0:64, co:co + 256], XB[:, kc, 0:64], Wv[:, kc, :],
                                                                    start=(kc == 0), stop=(kc == 7)),
                     reads=[("wr", slot), ("XB", kc)], writes=[psk(0)])
        P.op("act", lambda e: e.copy(out=YO[0:64, 0, 0:512], in_=PS[0][0:64, :]), reads=[psk(0)], writes=[("yo", 0)])
        P.op("pool", lambda e: e.memset(Vb[:, 7, :], 0.0), writes=["Vb"])
        P.op("pool", lambda e: e.tensor_copy(out=Vb[0:64, 7, :], in_=YO[0:64, 0, 0:512]), reads=[("yo", 0)], writes=["Vb"])
        for t in range(4):
            P.dma("yo0", lambda e, t=t: e.dma_start(out=cvs[:, WBUF - 4 + t, :], in_=YO[t * 16:(t + 1) * 16, 0, 0:512]),
                  reads=[("yo", 0)], writes=[("cvs_new", t)])
        for j in (0, 1, 2, 3):
            slot = wnext()
            Wv = WR[:, slot, 0:2048].rearrange("p (k m) -> p k m", k=8)
            for o2 in range(2):
                bank = 2 + o2
                for kc in range(8):
                    P.op("pe", lambda e, kc=kc, o2=o2, bank=bank, Wv=Wv: e.matmul(
                        PS[bank][:, cs], Wv[:, kc, o2 * 128:(o2 + 1) * 128], XB[:, kc, cs], start=(kc == 0), stop=(kc == 7)),
                        reads=[("wr", slot), ("XB", kc)], writes=[psk(bank)])
                if j < 2:
                    oc = j * 2 + o2
                    P.op("act", lambda e, oc=oc, bank=bank: e.activation(out=QT[:, oc, 0:64], in_=PS[bank][:, cs], func=AF.Copy, scale=0.125),
                         reads=[psk(bank)], writes=["QT"])
                else:
                    oc = (j - 2) * 2 + o2
                    P.op("act", lambda e, oc=oc, bank=bank: e.copy(out=QT[:, oc, 64:128], in_=PS[bank][:, cs]),
                         reads=[psk(bank)], writes=["QT"])
            if j >= 2:
                co = (j - 2) * 256
                for kc in range(8):
                    P.op("pe", lambda e, kc=kc, Wv=Wv, co=co: e.matmul(PS[1][0:64, co:co + 256], XB[:, kc, 0:64], Wv[:, kc, :],
                                                                        start=(kc == 0), stop=(kc == 7)),
                         reads=[("wr", slot), ("XB", kc)], writes=[psk(1)])
        P.op("act", lambda e: e.copy(out=YO[0:64, 1, 0:512], in_=PS[1][0:64, :]), reads=[psk(1)], writes=[("yo", 1)])
        for t in range(4):
            P.dma("yo1", lambda e, t=t: e.dma_start(out=cks[:, WBUF - 4 + t, :], in_=YO[t * 16:(t + 1) * 16, 1, 0:512]),
                  reads=[("yo", 1)], writes=[])
        sl8 = wnext()
        sl9 = wnext()
        W8 = WR[:, sl8, 0:2048].rearrange("p (k m) -> p k m", k=8)
        W9 = WR[:, sl9, 0:2048].rearrange("p (k m) -> p k m", k=8)
        for (Wv, slot, co) in ((W8, sl8, 0), (W9, sl9, 256)):
            for kc in range(8):
                P.op("pe", lambda e, kc=kc, Wv=Wv, co=co: e.matmul(PS[0][0:64, co:co + 256], XB[:, kc, 0:64], Wv[:, kc, :],
                                                                    start=(kc == 0), stop=(kc == 7)),
                     reads=[("wr", slot), ("XB", kc)], writes=[psk(0)])
        sgu_ln(0, 0, 64)
        for t in range(4):
            P.dma("sgo", lambda e, t=t: e.dma_start(out=sgo_v[t], in_=TMPB[t * 16:(t + 1) * 16, 0, :]), reads=[("TMPB", 0)], writes=[])
        for g in range(4):
            bank = 2 + g % 2
            P.op("pe", lambda e, g=g, bank=bank: e.matmul(PS[bank][:, cs], VNB[0:64, 0, g * 128:(g + 1) * 128], RS[0:64, g, :],
                                                          start=True, stop=True),
                 reads=[("VNB", 0), "RS"], writes=[psk(bank)])
            sgu_gate(g, bank, T, 4, sample=True)
        sgu_rms(T)
        P.op("pool", lambda e: e.memset(KTs[:, :, 896:1024], 0.0), writes=["KTs"])
        P.op("dve", lambda e: e.tensor_copy(out=KTs[:, :, 896:960], in_=QT[:, :, 64:128]), reads=["QT"], writes=["KTs"])
        for b in range(NB):
            sl = b % 2
            dsimple("ks%d_0" % sl, KSTG[:, sl, 0:4, :], ck[b, 1536:2048, :].rearrange("(t p) c -> p t c", p=128), writes=[kkeys[sl][0]])
            dsimple("vs%d_0" % sl, VSTG[:, sl, 0:4, :], cv[b, 1536:2048, :].rearrange("(t p) c -> p t c", p=128), writes=[vkeys[sl][0]])
            for t in range(4):
                dsimple("ks%d_%d" % (sl, t + 1), KSTG[32 * t:32 * t + 32, sl, 4:7, :], ck[b, t:1536:16, :].rearrange("(m i) c -> i m c", m=3),
                        writes=[kkeys[sl][1 + t]])
                dsimple("vs%d_%d" % (sl, t + 1), VSTG[32 * t:32 * t + 32, sl, 4:7, :], cv[b, t:1536:16, :].rearrange("(m i) c -> i m c", m=3),
                        writes=[vkeys[sl][1 + t]])
            P.op("pool", lambda e, sl=sl: e.tensor_copy(out=Vb[:, 0:7, :], in_=VSTG[:, sl, :, :]), reads=vkeys[sl], writes=["Vb"])
            for tile in range(7):
                bank = tile % 2
                for cc in range(4):
                    P.op("pe", lambda e, tile=tile, cc=cc, bank=bank, sl=sl: e.transpose(
                        out=PS[bank][:, cc * 128:(cc + 1) * 128], in_=KSTG[:, sl, tile, cc * 128:(cc + 1) * 128], identity=IDN[:]),
                        reads=kkeys[sl] + ["params"], writes=[psk(bank)])
                P.op("act", lambda e, tile=tile, bank=bank: e.copy(out=KTs[:, :, tile * 128:(tile + 1) * 128],
                                                                   in_=PS[bank][:].rearrange("p (c r) -> p c r", c=4)),
                     reads=[psk(bank)], writes=["KTs"])
            sbk = 2 + b % 2
            ptb = b % 2
            for tile in range(8):
                for h in range(8):
                    c_, hh = h // 2, h % 2
                    pr = slice(hh * 64, hh * 64 + 64)
                    col = tile * 32 + hh * 16 + c_ * 4
                    P.op("pe", lambda e, tile=tile, c_=c_, pr=pr, col=col, sbk=sbk, b=b: e.matmul(
                        PS[sbk][:, col:col + 4], KTs[pr, c_, tile * 128:(tile + 1) * 128], QT[pr, c_, b:64:16], start=True, stop=True),
                        reads=["KTs", "QT"], writes=[psk(sbk)])
            P.op("act", lambda e, sbk=sbk, ptb=ptb: e.activation(out=PT[:, ptb, 0:256], in_=PS[sbk][:, 0:256], func=AF.Exp),
                 reads=[psk(sbk)], writes=[("PT", ptb)])
            P.op("pool", lambda e, ptb=ptb: e.tensor_tensor(
                out=PT[:, ptb, 0:224].rearrange("p (a x t) -> p a x t", a=7, x=8),
                in0=PT[:, ptb, 0:224].rearrange("p (a x t) -> p a x t", a=7, x=8),
                in1=CNTB[:, 0:28].rearrange("p (a t) -> p a t", a=7).unsqueeze(2).to_broadcast([128, 7, 8, 4]), op=ALU.mult),
                reads=[("PT", ptb), "consts"], writes=[("PT", ptb)])
            P.op("pool", lambda e, ptb=ptb, b=b: e.tensor_tensor(
                out=PT[:, ptb, 224:256].rearrange("p (x t) -> p x t", x=8),
                in0=PT[:, ptb, 224:256].rearrange("p (x t) -> p x t", x=8),
                in1=CNT7B[:, b * 4:(b + 1) * 4].unsqueeze(1).to_broadcast([128, 8, 4]), op=ALU.mult),
                reads=[("PT", ptb), "consts"], writes=[("PT", ptb)])
            for hh in range(2):
                for tile in range(8):
                    lo = tile * 32 + hh * 16
                    P.op("pe", lambda e, tile=tile, lo=lo, hh=hh, ptb=ptb: e.matmul(
                        PS[4 + hh][0:16, :], PT[:, ptb, lo:lo + 16], Vb[:, tile, :], start=(tile == 0), stop=(tile == 7)),
                        reads=[("PT", ptb), "Vb"], writes=[psk(4 + hh)])
                for tile in range(8):
                    lo = tile * 32 + hh * 16
                    P.op("pe", lambda e, tile=tile, lo=lo, hh=hh, ptb=ptb: e.matmul(
                        PS[6][0:16, hh * 4:hh * 4 + 4], PT[:, ptb, lo:lo + 16], ONES1[:, 0:4], start=(tile == 0), stop=(tile == 7)),
                        reads=[("PT", ptb), "consts"], writes=[psk(6)])
            P.op("dve", lambda e: e.reciprocal(out=SMALL[0:16, 8:16], in_=PS[6][0:16, 0:8]), reads=[psk(6)], writes=["SMALL"])
            for hh in range(2):
                P.op("dve", lambda e, hh=hh: e.tensor_tensor(
                    out=TMPA[0:16, 0, 0:256].rearrange("p (c d) -> p c d", c=4),
                    in0=PS[4 + hh][0:16, :].rearrange("p (c h d) -> p c h d", c=4, h=2)[:, :, hh, :],
                    in1=MD.unsqueeze(2).to_broadcast([16, 4, 64]), op=ALU.mult),
                    reads=[psk(4 + hh), "params"], writes=[("TMPA", 0)])
                P.op("dve", lambda e, hh=hh: e.tensor_reduce(
                    out=TMPA[0:16, 1, hh * 64:(hh + 1) * 64], in_=TMPA[0:16, 0, 0:256].rearrange("p (c d) -> p d c", c=4),
                    axis=AX.X, op=ALU.add), reads=[("TMPA", 0)], writes=[("TMPA", 1)])
                P.op("dve", lambda e, hh=hh: e.tensor_scalar(
                    out=TMPB[0:16, 1, hh * 64:(hh + 1) * 64], in0=TMPA[0:16, 1, hh * 64:(hh + 1) * 64],
                    scalar1=SMALL[0:16, 8 + hh * 4:9 + hh * 4], scalar2=None, op0=ALU.mult),
                    reads=[("TMPA", 1), "SMALL"], writes=[("TMPB", 1)])
            P.op("pe", lambda e: e.transpose(out=PS[7][:, 0:16], in_=TMPB[0:16, 1, 0:128], identity=IDN[0:16, 0:16]),
                 reads=[("TMPB", 1), "params"], writes=[psk(7)])
            P.op("act", lambda e, b=b: e.copy(out=OBf[:, :, b:64:16], in_=PS[7][:, 0:16].rearrange("p (c t) -> p c t", c=4)),
                 reads=[psk(7)], writes=["OBf"])
        for c in range(4):
            P.op("act", lambda e, c=c: e.activation(out=ZQ[:, c % 2, cs], in_=OBf[:, c, cs], func=AF.Square),
                 reads=["OBf"], writes=[("ZQ", c % 2)])
            P.op("pe", lambda e, c=c: e.matmul(PS[3][:, cs], ONESB[:], ZQ[:, c % 2, cs], start=(c == 0), stop=(c == 3)),
                 reads=[("ZQ", c % 2), "consts"], writes=[psk(3)])
            P.op("act", lambda e, c=c: e.activation(out=OA[:, c, cs], in_=OBf[:, c, cs], func=AF.Identity, scale=OUTG[:, 0, c:c + 1]),
                 reads=["OBf", "params"], writes=["OA"])
        ra_from_ps3(T)
        wout_ln(T)
        P.alias(RKEYS_ATT, RKEYS_ACT)
        ffn(T, 4, 5, need_xb=False)
        for hb in range(2):
            bank = hb
            for d4 in range(4):
                dc = hb * 4 + d4
                P.op("pe", lambda e, dc=dc, d4=d4, bank=bank: e.transpose(out=PS[bank][0:64, d4 * 128:(d4 + 1) * 128],
                                                                           in_=XT[:, dc, 0:64], identity=IDN[:]),
                     reads=[("XT", dc), "params"], writes=[psk(bank)])
            P.op("act", lambda e, hb=hb, bank=bank: e.copy(out=YO[0:64, 0, hb * 512:(hb + 1) * 512], in_=PS[bank][0:64, :]),
                 reads=[psk(bank)], writes=[("yo", 0)])
        for t in range(4):
            P.dma("yo0", lambda e, t=t: e.dma_start(out=ys_v[t], in_=YO[t * 16:(t + 1) * 16, 0, :]), reads=[("yo", 0)], writes=[])

    if stage >= 1:
        for ti in range(NT):
            prompt_tile(ti)
        if with_sample:
            sample_tile()

    out_sems = ["yo0", "yo1"]
    with ExitStack() as es:
        sems = {}
        for e_ in ["pe", "act", "dve", "pool"]:
            sems[e_] = es.enter_context(nc.semaphore("c_" + e_))
        for k in P.dcnt:
            sems[k] = es.enter_context(nc.semaphore("d_" + str(k)))
        block = es.enter_context(nc.Block())

        def emitter(name):
            def f(eng):
                for waits, fn, inc in P.ops[name]:
                    for sem, val in waits:
                        eng.wait_ge(sems[sem], val)
                    ins = fn(eng)
                    ins.then_inc(sems[inc[0]], inc[1])
                if name == "sp":
                    for k in P.dcnt:
                        eng.wait_ge(sems[k], P.dcnt[k])
            return f
        block.tensor(emitter("pe"))
        block.scalar(emitter("act"))
        block.vector(emitter("dve"))
        block.gpsimd(emitter("pool"))
        block.sync(emitter("sp"))
    return nc


def _consts():
    k = np.arange(128)[:, None]
    q = np.arange(128)[None, :]
    return np.concatenate([np.eye(128), (k <= q), (k >= q)], axis=1).astype(np.float32)


def _consts2():
    out = np.zeros((128, 256), np.float32)
    part = np.arange(128)
    for tile in range(8):
        for t in range(4):
            if tile < 4:
                p = 1536 + 128 * tile + part
                valid = np.ones(128, bool)
            elif tile < 7:
                m = tile - 4
                tp, il = part // 32, part % 32
                p = 16 * (32 * m + il) + tp
                valid = np.ones(128, bool)
            else:
                p = 2048 + part
                valid = part < 4
            dist = 2048 + t - p
            c1 = (dist >= 0) & (dist <= 128)
            c4 = (dist >= 0) & (dist % 4 == 0) & (dist // 4 <= 128)
            c16 = (dist >= 0) & (dist % 16 == 0) & (dist // 16 <= 128)
            out[:, tile * 4 + t] = (c1.astype(np.float32) + c4 + c16) * valid
    r = np.arange(64)
    tj, bp = r // 16, r % 16
    ti, b = r // 16, r % 16
    out[0:64, 32:96] = ((bp[:, None] == b[None, :]) & (tj[:, None] <= ti[None, :])).astype(np.float32)
    q = np.arange(16)
    out[0:16, 96:100] = (q[:, None] // 4 == np.arange(4)[None, :]).astype(np.float32)
    for p in range(64):
        tp, bp_ = p // 16, p % 16
        for t in range(4):
            d = t - tp
            if d >= 0:
                out[p, 128 + bp_ * 4 + t] = 1.0 + (2.0 if d == 0 else 0.0)
    for k in range(4):
        out[k, 192 + k * 16:192 + (k + 1) * 16] = 1.0
    return out


_NC_CACHE = {}


def kernel(x_prompt, x_sample, cache_k, cache_v, ffn1_w_in, ffn1_w_out, ln1_g, ln1_b, w_in,
           sgu_w, sgu_b, sgu_v_g, sgu_v_b, out_a_g, out_b_g, w_out, ln2_g, ln2_b,
           ffn2_w_in, ffn2_w_out, ln3_g, ln3_b):
    f = lambda a: np.ascontiguousarray(np.asarray(a, dtype=np.float32))
    if "nc" not in _NC_CACHE:
        _NC_CACHE["nc"] = build_nc()
    nc = _NC_CACHE["nc"]
    shared = {
        "ffn1_w_in": f(ffn1_w_in)[0], "ffn2_w_in": f(ffn2_w_in)[0], "ffn1_w_out": f(ffn1_w_out)[0], "ffn2_w_out": f(ffn2_w_out)[0],
        "w_in": f(w_in)[0], "w_out": f(w_out)[0],
        "lnp": np.stack([f(ln1_g)[0], f(ln1_b)[0], f(ln2_g)[0], f(ln2_b)[0], f(ln3_g)[0], f(ln3_b)[0]]),
        "sgu_w": f(sgu_w)[0], "sgu_b": f(sgu_b)[0],
        "sgu_vgb": np.stack([f(sgu_v_g)[0], f(sgu_v_b)[0]]), "out_g": np.stack([f(out_a_g)[0], f(out_b_g)[0]]),
        "cmask": _consts(), "cmask2": _consts2(),
    }
    xpf, xsf, ckf, cvf = f(x_prompt), f(x_sample), f(cache_k)[0], f(cache_v)[0]
    in_maps = []
    for c in range(8):
        m = dict(shared)
        m["xp"] = xpf[c]
        m["xs"] = xsf[c * NB:(c + 1) * NB].reshape(64, D)
        m["ck"] = ckf[c * NB:(c + 1) * NB].reshape(NB, WBUF, 512)
        m["cv"] = cvf[c * NB:(c + 1) * NB].reshape(NB, WBUF, 512)
        in_maps.append(m)
    res = run_bass_kernel_spmd(nc, in_maps, core_ids=list(range(8)))
    R_ = res.results
    y_p = np.stack([R_[c]["yp"] for c in range(8)])
    y_s = np.concatenate([R_[c]["ys"].reshape(NB, 4, D) for c in range(8)])
    kpo = np.stack([R_[c]["kp"].reshape(2048, 8, 64) for c in range(8)])[None]
    vpo = np.stack([R_[c]["vp"].reshape(2048, 8, 64) for c in range(8)])[None]
    kso = np.concatenate([R_[c]["cks"].reshape(NB, WBUF, 8, 64) for c in range(8)])[None]
    vso = np.concatenate([R_[c]["cvs"].reshape(NB, WBUF, 8, 64) for c in range(8)])[None]
    sg = np.concatenate([R_[c]["sgo"].reshape(NB, 4, 512) for c in range(8)])[None]
    return (y_p, y_s, kpo, vpo, kso, vso, sg)
```
